# Optimizing a Trainium2 kernel written in Bass

```python
import jax, jax.numpy as jnp
from jax import lax
import numpy as np

D_MODEL = 2048
BATCH = 2
SEQ = 8192
DEPTH = 4

N_RET_LAYERS = DEPTH // 2
N_ATT_LAYERS = DEPTH - N_RET_LAYERS

RET_HEADS = 8
RET_QK_DIM = D_MODEL // RET_HEADS
RET_V_DIM = 2 * RET_QK_DIM
RET_CHUNK = 128
ROPE_BASE = 10000.0

DIL_CONFIGS = ((128, 1), (512, 4), (2048, 16))
N_GROUPS = len(DIL_CONFIGS)
ATT_HEAD_DIM = 128
ATT_HEADS = D_MODEL // ATT_HEAD_DIM
REL_BUCKETS = 32
REL_MAX_DIST = 2048
FFN_DIM = -(-8 * D_MODEL // (3 * 256)) * 256
NORM_EPS = 1e-6
NEG_INF = -1e30

kernel_name = "yoco_retention_dilated_attention_trunk"


def _rms(x, g):
    xf = x.astype(jnp.float32)
    y = xf * lax.rsqrt(jnp.mean(xf * xf, axis=-1, keepdims=True) + NORM_EPS)
    return (y * g.astype(jnp.float32)).astype(x.dtype)


def _swiglu(h, w_in, w_out):
    z = h @ w_in
    return (jax.nn.silu(z[..., :FFN_DIM]) * z[..., FFN_DIM:]) @ w_out


def _rope(t, cos, sin):
    half = t.shape[-1] // 2
    t1, t2 = t[..., :half], t[..., half:]
    return jnp.concatenate([t1 * cos - t2 * sin, t2 * cos + t1 * sin], axis=-1)


def _retention(q, k, v):
    B_, S_, H_, dk = q.shape
    dv = v.shape[-1]
    C = RET_CHUNK
    N = S_ // C
    log_g = np.log(1.0 - 2.0 ** (-5.0 - np.arange(H_))).astype(np.float32)
    idx = np.arange(C)
    diff = idx[:, None] - idx[None, :]
    dmask = np.where(diff[None] >= 0, np.exp(log_g[:, None, None] * np.maximum(diff, 0)[None]), 0.0)
    q_dec = np.exp(log_g[None, :] * (idx[:, None] + 1))
    k_dec = np.exp(log_g[None, :] * (C - 1 - idx)[:, None])
    c_dec = np.exp(log_g * C)
    dmask = jnp.asarray(dmask, dtype=q.dtype)
    q_dec = jnp.asarray(q_dec, dtype=q.dtype)
    k_dec = jnp.asarray(k_dec, dtype=q.dtype)
    c_dec = jnp.asarray(c_dec, dtype=q.dtype)

    def chunks(t):
        return jnp.moveaxis(t.reshape(B_, N, C, H_, t.shape[-1]), 1, 0)

    def step(R, inp):
        qc, kc, vc = inp
        s = jnp.einsum('bihd,bjhd->bhij', qc, kc) * dmask
        inner = jnp.einsum('bhij,bjhe->bihe', s, vc)
        cross = jnp.einsum('bihd,bhde->bihe', qc, R) * q_dec[None, :, :, None]
        R = R * c_dec[None, :, None, None] + jnp.einsum('bjhd,bjhe->bhde', kc * k_dec[None, :, :, None], vc)
        return R, inner + cross

    R0 = jnp.zeros((B_, H_, dk, dv), q.dtype)
    _, y = lax.scan(step, R0, (chunks(q), chunks(k), chunks(v)))
    return jnp.moveaxis(y, 0, 1).reshape(B_, S_, H_, dv)


def _retention_layer(h, w_in, w_out):
    B_, S_, _ = h.shape
    nq = RET_HEADS * RET_QK_DIM
    nv = RET_HEADS * RET_V_DIM
    z = h @ w_in
    q = z[..., :nq].reshape(B_, S_, RET_HEADS, RET_QK_DIM)
    k = z[..., nq:2 * nq].reshape(B_, S_, RET_HEADS, RET_QK_DIM) * (RET_QK_DIM ** -0.5)
    v = z[..., 2 * nq:2 * nq + nv].reshape(B_, S_, RET_HEADS, RET_V_DIM)
    g = z[..., 2 * nq + nv:]
    inv = (1.0 / ROPE_BASE ** np.linspace(0.0, 1.0, RET_QK_DIM // 2)).astype(np.float32)
    ang = jnp.arange(S_, dtype=jnp.float32)[:, None] * jnp.asarray(inv)[None, :]
    cos = jnp.cos(ang)[:, None, :].astype(h.dtype)
    sin = jnp.sin(ang)[:, None, :].astype(h.dtype)
    y = _retention(_rope(q, cos, sin), _rope(k, cos, sin), v)
    yf = y.astype(jnp.float32)
    y = (yf * lax.rsqrt(jnp.mean(yf * yf, axis=-1, keepdims=True) + NORM_EPS)).astype(h.dtype)
    y = y.reshape(B_, S_, nv) * jax.nn.silu(g)
    return y @ w_out


def _t5_bucket(dist):
    n = np.maximum(dist, 0)
    max_exact = REL_BUCKETS // 2
    large = max_exact + (np.log(np.maximum(n, 1) / max_exact) / np.log(REL_MAX_DIST / max_exact)
                         * (REL_BUCKETS - max_exact)).astype(np.int32)
    large = np.minimum(large, REL_BUCKETS - 1)
    return np.where(n < max_exact, n, large).astype(np.int32)


def _to_strided(t, d, blk):
    B_, S_, H_, E_ = t.shape
    L = S_ // d
    nb = -(-L // blk)
    t = t.reshape(B_, L, d, H_, E_).transpose(0, 2, 1, 3, 4)
    t = jnp.pad(t, ((0, 0), (0, 0), (0, nb * blk - L), (0, 0), (0, 0)))
    return t.reshape(B_, d, nb, blk, H_, E_)


def _from_strided(t, S_):
    B_, d, nb, blk = t.shape[:4]
    rest = t.shape[4:]
    t = t.reshape((B_, d, nb * blk) + rest)[:, :, :S_ // d]
    t = jnp.moveaxis(t, 1, 2)
    return t.reshape((B_, S_) + rest)


def _band(t):
    prev = jnp.pad(t[:, :, :-1], ((0, 0), (0, 0), (1, 0), (0, 0), (0, 0), (0, 0)))
    return jnp.concatenate([prev, t], axis=3)


def _shared_kv(x, g_kv, w_kv, rel_bias):
    B_, S_, _ = x.shape
    kv = (_rms(x, g_kv) @ w_kv).reshape(B_, S_, 2, N_GROUPS, ATT_HEADS, ATT_HEAD_DIM)
    shared, patterns = [], []
    for gi, (window, d) in enumerate(DIL_CONFIGS):
        blk = window // d
        nb = -(-(S_ // d) // blk)
        kb = _to_strided(kv[:, :, 0, gi], d, blk)
        vb = _to_strided(kv[:, :, 1, gi], d, blk)
        i = np.arange(blk)[:, None]
        c = np.arange(2 * blk)[None, :]
        delta = blk + i - c
        band = (delta >= 0) & (delta <= blk)
        bucket = _t5_bucket(np.maximum(delta, 0) * d)
        table_g = rel_bias[:, gi * ATT_HEADS:(gi + 1) * ATT_HEADS].astype(jnp.float32)
        bias = jnp.moveaxis(table_g[bucket], -1, 0)
        mask = band[None] & ((np.arange(nb)[:, None, None] > 0) | (c >= blk)[None])
        shared.append((kb, vb))
        patterns.append((bias, jnp.asarray(mask)))
    return shared, patterns


def _dilated_group(q, kb, vb, bias, mask, d):
    S_ = q.shape[1]
    blk = kb.shape[3]
    qb = _to_strided(q, d, blk)
    kband = _band(kb)
    vband = _band(vb)
    s = jnp.einsum('brnihe,brnjhe->brnhij', qb, kband).astype(jnp.float32) * (ATT_HEAD_DIM ** -0.5) + bias
    s = jnp.where(mask[None, None, :, None], s, NEG_INF)
    m = jnp.max(s, axis=-1, keepdims=True)
    p = jnp.exp(s - m)
    den = jnp.sum(p, axis=-1, keepdims=True)
    o = jnp.einsum('brnhij,brnjhe->brnihe', (p / den).astype(vb.dtype), vband)
    lse = jnp.moveaxis((m + jnp.log(den))[..., 0], 3, 4)
    return _from_strided(o, S_), _from_strided(lse, S_)


def _dilated_layer(h, w_q, w_out, shared, patterns):
    B_, S_, _ = h.shape
    q = (h @ w_q).reshape(B_, S_, N_GROUPS, ATT_HEADS, ATT_HEAD_DIM)
    outs, lses = [], []
    for gi, (window, d) in enumerate(DIL_CONFIGS):
        kb, vb = shared[gi]
        bias, mask = patterns[gi]
        o, lse = _dilated_group(q[:, :, gi], kb, vb, bias, mask, d)
        outs.append(o)
        lses.append(lse)
    wts = jax.nn.softmax(jnp.stack(lses, axis=0), axis=0)
    o = jnp.einsum('gbsh,gbshe->bshe', wts.astype(h.dtype), jnp.stack(outs, axis=0))
    return o.reshape(B_, S_, ATT_HEADS * ATT_HEAD_DIM) @ w_out


def setup_inputs(seed: int = 0) -> dict:
    key = jax.random.key(seed)
    ks = jax.random.split(key, 13)
    res = (2.0 * DEPTH) ** -0.5
    nq = RET_HEADS * RET_QK_DIM
    nv = RET_HEADS * RET_V_DIM
    natt = N_GROUPS * ATT_HEADS * ATT_HEAD_DIM

    def nrm(k, shape, fan_in, scale=1.0):
        return jax.random.normal(k, shape, jnp.float32) * (scale * fan_in ** -0.5)

    def gain(k, shape):
        return 1.0 + 0.05 * jax.random.normal(k, shape, jnp.float32)

    return {
        "x": jax.random.normal(ks[0], (BATCH, SEQ, D_MODEL), jnp.float32),
        "g_mix": gain(ks[1], (DEPTH, D_MODEL)),
        "g_ffn": gain(ks[2], (DEPTH, D_MODEL)),
        "w_ret_in": nrm(ks[3], (N_RET_LAYERS, D_MODEL, 2 * nq + 2 * nv), D_MODEL),
        "w_ret_out": nrm(ks[4], (N_RET_LAYERS, nv, D_MODEL), nv, res),
        "g_kv": gain(ks[5], (D_MODEL,)),
        "w_kv": nrm(ks[6], (D_MODEL, 2 * natt), D_MODEL),
        "w_att_q": nrm(ks[7], (N_ATT_LAYERS, D_MODEL, natt), D_MODEL),
        "w_att_out": nrm(ks[8], (N_ATT_LAYERS, ATT_HEADS * ATT_HEAD_DIM, D_MODEL), ATT_HEADS * ATT_HEAD_DIM, res),
        "rel_bias": 0.1 * jax.random.normal(ks[9], (REL_BUCKETS, N_GROUPS * ATT_HEADS), jnp.float32),
        "w_ffn_in": nrm(ks[10], (DEPTH, D_MODEL, 2 * FFN_DIM), D_MODEL),
        "w_ffn_out": nrm(ks[11], (DEPTH, FFN_DIM, D_MODEL), FFN_DIM, res),
        "g_final": gain(ks[12], (D_MODEL,)),
    }


def reference(x, g_mix, g_ffn, w_ret_in, w_ret_out, g_kv, w_kv, w_att_q, w_att_out, rel_bias, w_ffn_in, w_ffn_out, g_final):
    shared, patterns = None, None
    for l in range(DEPTH):
        if l < N_RET_LAYERS:
            x = x + _retention_layer(_rms(x, g_mix[l]), w_ret_in[l], w_ret_out[l])
        else:
            if l == N_RET_LAYERS:
                shared, patterns = _shared_kv(x, g_kv, w_kv, rel_bias)
            j = l - N_RET_LAYERS
            x = x + _dilated_layer(_rms(x, g_mix[l]), w_att_q[j], w_att_out[j], shared, patterns)
        x = x + _swiglu(_rms(x, g_ffn[l]), w_ffn_in[l], w_ffn_out[l])
    return _rms(x, g_final)
```

```python
import contextlib
import math
import os
import numpy as np
import ml_dtypes
import concourse.bass as bass
import concourse.mybir as mybir
from concourse.bass_utils import run_bass_kernel_spmd

F32 = mybir.dt.float32
BF16 = mybir.dt.bfloat16
AF = mybir.ActivationFunctionType
ALU = mybir.AluOpType
AX = mybir.AxisListType
ENGS = ("pe", "act", "dve", "pool", "sp")

D = 2048
TOK = 2048
NQ = 2048
NV = 4096
FFN = 5632
EPS = 1e-6
DIL = (1, 4, 16)
NCG = tuple(d * (128 + TOK // d) for d in DIL)
HALO = 128 + 512 + 2048
HOFF = (0, 128, 640)


class Op:
    __slots__ = ("eng", "fn", "deps", "dma_sem", "inc", "idx", "signal", "sigval")


class Sched:
    def __init__(self, nc):
        self.nc = nc
        self.ops = []
        self.last_w = {}
        self.readers = {}
        self.dma_val = {}
        self.last_on = {}
        self.stack = contextlib.ExitStack()

    def sb(self, name, shape, dt):
        return self.stack.enter_context(self.nc.sbuf_tensor(name, list(shape), dt))

    def ps(self, name, shape, dt=F32):
        return self.stack.enter_context(self.nc.psum_tensor(name, list(shape), dt))

    def add(self, eng, fn, reads=(), writes=(), dma_sem=None, inc=16, extra=(), nolast=False):
        op = Op()
        op.eng, op.fn, op.dma_sem, op.inc = eng, fn, dma_sem, inc
        op.signal, op.sigval = False, 0
        op.idx = len(self.ops)
        deps = set(extra)
        for k in reads:
            w = self.last_w.get(k)
            if w is not None:
                deps.add(w)
        for k in writes:
            w = self.last_w.get(k)
            if w is not None:
                deps.add(w)
            r = self.readers.get(k)
            if r:
                deps.update(r.values())
        op.deps = []
        for d in deps:
            dop = self.ops[d]
            if dop.dma_sem is not None:
                op.deps.append((d, self.dma_val[dop.dma_sem]))
            elif dop.eng != eng:
                op.deps.append((d, None))
        if dma_sem is not None:
            self.dma_val[dma_sem] = self.dma_val.get(dma_sem, 0) + inc
        rk = (eng, dma_sem)
        for k in reads:
            self.readers.setdefault(k, {})[rk] = op.idx
        for k in writes:
            self.last_w[k] = op.idx
            self.readers[k] = {}
        self.ops.append(op)
        if not nolast:
            self.last_on[rk] = op.idx
        return op

    def barrier(self, exclude=()):
        lasts = [v for (e_, ds_), v in self.last_on.items() if ds_ not in exclude]
        for e in ENGS:
            self.add(e, lambda eng: eng.nop(), extra=lasts, nolast=True)

    def emit(self, final_sems=()):
        nc, ops = self.nc, self.ops
        for op in ops:
            for d, v in op.deps:
                if v is None:
                    ops[d].signal = True
        counters = {e: 0 for e in ENGS}
        for op in ops:
            if op.dma_sem is None and op.signal:
                counters[op.eng] += 1
                op.sigval = counters[op.eng]
        with contextlib.ExitStack() as st:
            esem = {e: st.enter_context(nc.semaphore("s_" + e)) for e in ENGS}
            dsem = {n: st.enter_context(nc.semaphore("d_" + n)) for n in self.dma_val}
            block = st.enter_context(nc.Block())
            by_eng = {e: [o for o in ops if o.eng == e] for e in ENGS}

            def run(engname, eng):
                waited = {}
                for op in by_eng[engname]:
                    for d, v in op.deps:
                        dop = ops[d]
                        if v is None:
                            sem, val, key = esem[dop.eng], dop.sigval, dop.eng
                        else:
                            sem, val, key = dsem[dop.dma_sem], v, "d_" + dop.dma_sem
                        if waited.get(key, 0) < val:
                            eng.wait_ge(sem, val)
                            waited[key] = val
                    ins = op.fn(eng)
                    if op.dma_sem is not None:
                        ins.then_inc(dsem[op.dma_sem], op.inc)
                    elif op.signal:
                        ins.then_inc(esem[engname], 1)
                if engname == "sp":
                    for n in final_sems:
                        eng.wait_ge(dsem[n], self.dma_val[n])

            block.tensor(lambda e: run("pe", e))
            block.scalar(lambda e: run("act", e))
            block.vector(lambda e: run("dve", e))
            block.gpsimd(lambda e: run("pool", e))
            block.sync(lambda e: run("sp", e))
        self.stack.close()


ARENA = 192 * 1024
SM = 184 * 1024
IN_OFF, W_OFF, E_OFF = 0, 88 * 1024, 152 * 1024
WSLOT = 32 * 1024


class Builder:
    def __init__(self, stop_after=None, debug_outs=()):
        self.nc = nc = bass.Bass("TRN2", target_bir_lowering=False)
        self.S = S = Sched(nc)
        self.stop_after = stop_after
        self.debug_outs = set(debug_outs)
        self.dr = {}
        self.arena = S.sb("arena", [128, ARENA // 2], BF16)
        self.pss = [S.ps(f"ps{i}", [128, 512], F32) for i in range(8)]
        self.bank = 0
        self.uid = 0

    def dram(self, name, shape, dt, kind="Internal"):
        if name in self.debug_outs:
            kind = "ExternalOutput"
        t = self.nc.dram_tensor(name, list(shape), dt, kind=kind).ap()
        self.dr[name] = t
        return t

    def carve(self, off, shape, dt):
        n = int(np.prod(shape))
        esz = 4 if dt == F32 else 2
        assert off % 4 == 0 and off + n * esz <= ARENA, (off, shape)
        a = self.arena[:, off // 2:(off + n * esz) // 2]
        if dt == F32:
            a = a.bitcast(F32)
        if len(shape) == 2:
            a = a.rearrange("p (a b) -> p a b", a=shape[0])
        elif len(shape) == 3:
            a = a.rearrange("p (a b c) -> p a b c", a=shape[0], b=shape[1])
        return a

    def nb(self, n=1):
        b = self.bank
        self.bank = (self.bank + n) % 8
        return b

    def key(self, s):
        self.uid += 1
        return (s, self.uid)


def build_program(stop_after=None, debug_outs=()):
    B = Builder(stop_after, debug_outs)
    nc, S = B.nc, B.S
    add = S.add
    pss = B.pss

    def ext(name, shape, dt=F32):
        return nc.dram_tensor(name, list(shape), dt, kind="ExternalInput").ap()

    xT_in = ext("xT", [D, TOK])
    WSPEC = dict(w_ret_in=(2, D, 12288), w_ret_out=(2, NV, D), w_kv=(1, D, 12288), w_att_q=(2, D, 6144),
                 w_att_out=(2, D, D), w_ffn_in=(4, D, 2 * FFN), w_ffn_out=(4, FFN, D))
    WFULL = {}

    def gather_weight(name, l):
        L, R, N = WSPEC[name]
        if (name, l) in WFULL:
            return WFULL[(name, l)]
        if SHARD_W:
            shard = ext(f"{name}_{l}", [R // 8, N])
            stage = nc.dram_tensor(f"st_{name}_{l}", [R // 8, N], F32, kind="Internal").ap()
            full = nc.dram_tensor(f"fu_{name}_{l}", [R, N], F32, kind="Internal").ap()
            add("sp", lambda e: e.dma_start(out=stage, in_=shard), writes=[("wst", name, l)], dma_sem="wst")
            add("pool", lambda e: e.collective_compute("AllGather", ALU.bypass, replica_groups=[list(range(8))],
                                                        ins=[stage], outs=[full]),
                reads=[("wst", name, l)], writes=[("wfu", name, l)], dma_sem="wcc", inc=1)
        else:
            full = ext(f"{name}_{l}", [R, N])
        WFULL[(name, l)] = (full, ("wfu", name, l))
        return WFULL[(name, l)]
    gcols_d = ext("gcols", [128, 10 * 16])
    cos_d = ext("cosT", [128, TOK])
    sin_d = ext("sinT", [128, TOK])
    cst_d = ext("cst", [128, CST_N])
    cstb_d = ext("cstb", [128, 256], BF16)
    bias_d = ext("biasT", [16, 128, 6 * 256])
    outT = nc.dram_tensor("outT", [D, TOK], F32, kind="ExternalOutput").ap()

    xT = B.dram("x_res", [D, TOK], F32)
    qT_d = B.dram("qT", [NQ, TOK], BF16)
    kT_d = B.dram("kT", [NQ, TOK], BF16)
    v_d = B.dram("v_tok", [TOK, NV], BF16)
    sg_d = B.dram("sgT", [NV, TOK], BF16)
    yT_d = B.dram("yT", [NV, TOK], BF16)
    aT_d = B.dram("aT", [FFN, TOK], BF16)
    rfin_d = B.dram("rfin", [2048, 512], F32)
    rall_d = B.dram("rall", [8 * 2048, 512], F32)
    Kc_d = [B.dram(f"Kc{g}", [D, NCG[g]], BF16) for g in range(3)]
    Vc_d = [B.dram(f"Vc{g}", [NCG[g], D], BF16) for g in range(3)]
    KH_d = B.dram("KH", [D, HALO], BF16)
    VH_d = B.dram("VH", [HALO, D], BF16)
    VPART = (0, 768, 1408, 2048, 2688)
    KHall_p = [B.dram(f"KHall{p_}", [8 * 512, HALO], BF16) for p_ in range(4)]
    VHall_p = [B.dram(f"VHall{p_}", [8 * (VPART[p_ + 1] - VPART[p_]), D], BF16) for p_ in range(4)]
    qa_d = [B.dram(f"qa{g}", [D, TOK], BF16) for g in range(3)]
    oT_d = B.dram("oT", [D, TOK], BF16)
    lse_d = B.dram("lseD", [3, TOK, 16], F32)
    LSE_d = B.dram("LSEd", [TOK, 16], F32)

    gcols = S.sb("gcols_sb", [128, 10, 16], F32)
    cst = S.sb("cst_sb", [128, CST_N], F32)
    cstb = S.sb("cstb_sb", [128, 256], BF16)
    ident = cstb[:, 0:128]
    ones = cstb[:, 128:256]
    add("sp", lambda e: e.dma_start(out=gcols[:], in_=gcols_d.rearrange("p (a b) -> p a b", a=10)),
        writes=["gcols"], dma_sem="c0")
    add("sp", lambda e: e.dma_start(out=cst[:], in_=cst_d), writes=["cst"], dma_sem="c0")
    add("sp", lambda e: e.dma_start(out=cstb[:], in_=cstb_d), writes=["cstb"], dma_sem="c0")
    CONSTK = ["gcols", "cst", "cstb"]

    def cview(name):
        o, shp = CST_LAYOUT[name]
        n = int(np.prod(shp))
        a = cst[:, o:o + n]
        if len(shp) == 2:
            a = a.rearrange("p (a b) -> p a b", a=shp[0])
        return a

    dmaskT = cview("dmaskT")
    qdec = cview("qdec")
    kdec = cview("kdec")
    decA = cview("decA")
    coefR = cview("coefR")
    maskH = cview("maskH")

    IN = lambda shape, dt, off=0: B.carve(IN_OFF + off, shape, dt)
    WS = lambda slot, shape, dt, off=0: B.carve(W_OFF + slot * WSLOT + off, shape, dt)
    EE = lambda off, shape, dt: B.carve(E_OFF + off, shape, dt)

    XK = [("x", c, t) for c in range(16) for t in range(4)]
    add("sp", lambda e: e.dma_start(out=xT, in_=xT_in), writes=XK, dma_sem="xcp")

    def norm_phase(gidx, final=False):
        S.barrier()
        hT = IN([16, TOK], BF16)
        TB = 256
        for tb in range(TOK // TB):
            slot = tb % 2
            xs = WS(slot, [16, TB], F32)
            sq = WS(slot, [16, TB], BF16, off=16384)
            rs = IN([TB], F32, off=80 * 1024 + slot * 1024)
            add("sp", lambda e, xs=xs, tb=tb: e.dma_start(
                out=xs, in_=xT.rearrange("(c p) t -> p c t", p=128)[:, :, tb * TB:(tb + 1) * TB]),
                reads=[("x", c, tb // 2) for c in range(16)], writes=[("w", slot)], dma_sem=f"w{slot}")
            add("act", lambda e, xs=xs, sq=sq: e.activation(out=sq, in_=xs, func=AF.Square),
                reads=[("w", slot)], writes=[("w", slot)])
            b = B.nb()
            for c in range(16):
                add("pe", lambda e, c=c, b=b, sq=sq: e.matmul(pss[b][:, 0:TB], lhsT=ones, rhs=sq[:, c, :],
                                                                 start=(c == 0), stop=(c == 15)),
                    reads=[("w", slot), "cstb"], writes=[("ps", b)])
            add("act", lambda e, b=b, rs=rs: e.activation(out=rs, in_=pss[b][:, 0:TB], func=AF.Sqrt,
                                                           bias=epsc[:, 0:1], scale=1.0 / D),
                reads=[("ps", b), "cst"], writes=[("rs", slot)])
            add("dve", lambda e, rs=rs: e.reciprocal(out=rs, in_=rs), reads=[("rs", slot)], writes=[("rs", slot)])
            if not final:
                for c in range(16):
                    eng = "dve"
                    add(eng, lambda e, c=c, xs=xs, rs=rs, tb=tb: e.scalar_tensor_tensor(
                        out=hT[:, c, tb * TB:(tb + 1) * TB], in0=xs[:, c, :], scalar=gcols[:, gidx, c:c + 1],
                        in1=rs, op0=ALU.mult, op1=ALU.mult),
                        reads=[("w", slot), ("rs", slot), "gcols"], writes=[("in", c)])
            else:
                for c in range(16):
                    eng = "dve"
                    add(eng, lambda e, c=c, xs=xs, rs=rs: e.scalar_tensor_tensor(
                        out=xs[:, c, :], in0=xs[:, c, :], scalar=gcols[:, gidx, c:c + 1],
                        in1=rs, op0=ALU.mult, op1=ALU.mult),
                        reads=[("rs", slot), "gcols"], writes=[("w", slot)])
                add("sp", lambda e, xs=xs, tb=tb: e.dma_start(
                    out=outT.rearrange("(c p) t -> p c t", p=128)[:, :, tb * TB:(tb + 1) * TB], in_=xs),
                    reads=[("w", slot)], writes=["outT"], dma_sem=f"w{slot}")
        return hT

    epsc = cview("eps")

    def gemm(Wd, KC, groups, rhs_fn=None, NTB=4, epi=None, lhs_fn=None, NTT=16, epi_tok=None,
             in_keys=None):
        if in_keys is None:
            in_keys = [("in", c) for c in range(KC)]
        Wd, wkey = Wd
        Wv = Wd.rearrange("(c p) n -> p c n", p=128)

        def load(gi):
            g = groups[gi]
            slot = gi % 2
            ncols = sum(n for _, n in g["cols"])
            wb = WS(slot, [KC, ncols], BF16)
            o = 0
            for (c0, n) in g["cols"]:
                add("pool", lambda e, wb=wb, o=o, c0=c0, n=n: e.dma_start(out=wb[:, :, o:o + n],
                                                                          in_=Wv[:, :, c0:c0 + n]),
                    reads=[wkey], writes=[("w", slot)], dma_sem=f"w{slot}")
                o += n
            return wb

        wbs = {0: load(0)}
        for gi, g in enumerate(groups):
            if gi + 1 < len(groups):
                wbs[gi + 1] = load(gi + 1)
            wb = wbs.pop(gi)
            slot = gi % 2
            if g.get("tok"):
                ncols = sum(n for _, n in g["cols"])
                for tt in range(NTT):
                    b = B.nb()
                    for k in range(KC):
                        add("pe", lambda e, k=k, b=b, tt=tt, wb=wb, ncols=ncols, gi=gi: e.matmul(
                            pss[b][:, 0:ncols], lhsT=lhs_fn(k, tt, gi), rhs=wb[:, k, :], start=(k == 0),
                            stop=(k == KC - 1)),
                            reads=[("w", slot), in_keys[k]], writes=[("ps", b)])
                    epi_tok(gi, tt, b)
            else:
                for si, st in enumerate(g["sets"]):
                    for tp in range(0, NTB, 2):
                        tbs = list(range(tp, min(tp + 2, NTB)))
                        banks = {}
                        for ch in st:
                            for tb in tbs:
                                banks[(ch, tb)] = B.nb()
                        for ch in st:
                            for k in range(KC):
                                for tb in tbs:
                                    b = banks[(ch, tb)]
                                    add("pe", lambda e, k=k, b=b, tb=tb, ch=ch, wb=wb, gi=gi: e.matmul(
                                        pss[b][:], lhsT=wb[:, k, ch * 128:(ch + 1) * 128], rhs=rhs_fn(k, tb, gi),
                                        start=(k == 0), stop=(k == KC - 1)),
                                        reads=[("w", slot), in_keys[k]], writes=[("ps", b)])
                        for tb in tbs:
                            epi(gi, si, tb, [banks[(ch, tb)] for ch in st])

    def make_resid_epi(chunk_of, t0):
        st = {"n": 0}

        def epi(gi, si, tb, banks):
            ch = chunk_of(gi, si)
            i = st["n"] % 4
            st["n"] += 1
            xo = EE(i * 2048, [512], F32)
            c0 = t0 + tb * 512
            xk = ("x", ch, c0 // 512)
            add("sp", lambda e: e.dma_start(out=xo, in_=xT[ch * 128:(ch + 1) * 128, c0:c0 + 512]),
                reads=[xk], writes=[("xo", i)], dma_sem=f"xo{i}")
            b = banks[0]
            add("dve", lambda e: e.tensor_tensor(out=xo, in0=pss[b][:], in1=xo, op=ALU.add),
                reads=[("ps", b), ("xo", i)], writes=[("xo", i)])
            add("sp", lambda e: e.dma_start(out=xT[ch * 128:(ch + 1) * 128, c0:c0 + 512], in_=xo),
                reads=[("xo", i)], writes=[xk], dma_sem=f"xo{i}")
        return epi

    def ffn_layer(l):
        hT = norm_phase(4 + l)
        Wi = WFULL[('w_ffn_in', l)]
        groups = [dict(cols=[(j * 256, 256), (FFN + j * 256, 256)], sets=[[0, 2], [1, 3]]) for j in range(22)]
        cnt = {"n": 0}

        def epi1(gi, si, tb, banks):
            i = cnt["n"] % 2
            cnt["n"] += 1
            sg = EE(8192 + i * 2048, [512], F32)
            ab = EE(12288 + i * 1024, [512], BF16)
            bg, bu = banks
            add("act", lambda e: e.activation(out=sg, in_=pss[bg][:], func=AF.Silu),
                reads=[("ps", bg)], writes=[("sg", i)])
            add("dve", lambda e: e.tensor_tensor(out=ab, in0=pss[bu][:], in1=sg, op=ALU.mult),
                reads=[("ps", bu), ("sg", i)], writes=[("ab", i)])
            r0 = gi * 256 + si * 128
            add("sp", lambda e: e.dma_start(out=aT_d[r0:r0 + 128, tb * 512:(tb + 1) * 512], in_=ab),
                reads=[("ab", i)], writes=[("aT", r0 // 128)], dma_sem=f"ab{i}")

        gemm(Wi, 16, groups, rhs_fn=lambda k, tb, gi: hT[:, k, tb * 512:(tb + 1) * 512], NTB=4, epi=epi1)
        Wo = WFULL[('w_ffn_out', l)]
        for ps_ in range(2):
            t0 = ps_ * 1024
            aT = IN([44, 1024], BF16)
            for q4 in range(4):
                add("sp", lambda e, q4=q4, t0=t0: e.dma_start(
                    out=aT[:, q4 * 11:(q4 + 1) * 11, :],
                    in_=aT_d.rearrange("(c p) t -> p c t", p=128)[:, q4 * 11:(q4 + 1) * 11, t0:t0 + 1024]),
                    reads=[("aT", c) for c in range(q4 * 11, (q4 + 1) * 11)],
                    writes=[("in", c) for c in range(44)], dma_sem="inl")
            groups2 = [dict(cols=[(j * 256, 256)], sets=[[0], [1]]) for j in range(8)]
            gemm(Wo, 44, groups2, rhs_fn=lambda k, tb, gi: aT[:, k, tb * 512:(tb + 1) * 512], NTB=2,
                 epi=make_resid_epi(lambda gi, si: gi * 2 + si, t0))

    def ret_layer(l):
        hT = norm_phase(l)
        cosT = IN([TOK], F32, off=64 * 1024)
        sinT = IN([TOK], F32, off=72 * 1024)
        add("sp", lambda e: e.dma_start(out=cosT, in_=cos_d), writes=["cos"], dma_sem="c0")
        add("sp", lambda e: e.dma_start(out=sinT, in_=sin_d), writes=["sin"], dma_sem="c0")
        Wi = WFULL[('w_ret_in', l)]
        groups = []
        for j in range(8):
            groups.append(dict(cols=[(j * 512, 512)], sets=[[0, 1], [2, 3]], kind="qk"))
        for j in range(8):
            groups.append(dict(cols=[(8192 + j * 512, 512)], sets=[[0], [1], [2], [3]], kind="g"))
        for j in range(8):
            groups.append(dict(cols=[(4096 + j * 512, 512)], tok=True, kind="v"))
        cnt = {"n": 0, "g": 0, "v": 0}

        def epi(gi, si, tb, banks):
            g = groups[gi]
            if g["kind"] == "qk":
                isk = gi >= 4
                dst = kT_d if isk else qT_d
                row0 = (gi % 4) * 512 + si * 256
                i = cnt["n"] % 2
                cnt["n"] += 1
                a1 = EE(0 + i * 4096, [512], F32)
                a2 = EE(2048 + i * 4096, [512], F32)
                u = EE(8192 + i * 4096, [512], F32)
                w_ = EE(10240 + i * 4096, [512], F32)
                n1 = EE(16384 + i * 2048, [512], BF16)
                n2 = EE(17408 + i * 2048, [512], BF16)
                b1, b2 = banks
                sc = (1.0 / 16.0) if isk else 1.0
                cs = cosT[:, tb * 512:(tb + 1) * 512]
                sn = sinT[:, tb * 512:(tb + 1) * 512]
                add("act", lambda e: e.activation(out=a1, in_=pss[b1][:], func=AF.Copy, scale=sc),
                    reads=[("ps", b1)], writes=[("a1", i)])
                add("act", lambda e: e.activation(out=a2, in_=pss[b2][:], func=AF.Copy, scale=sc),
                    reads=[("ps", b2)], writes=[("a2", i)])
                add("dve", lambda e: e.tensor_tensor(out=u, in0=a1, in1=cs, op=ALU.mult),
                    reads=[("a1", i), "cos"], writes=[("u", i)])
                add("pool", lambda e: e.tensor_tensor(out=w_, in0=a2, in1=sn, op=ALU.mult),
                    reads=[("a2", i), "sin"], writes=[("w_", i)])
                add("dve", lambda e: e.tensor_tensor(out=n1, in0=u, in1=w_, op=ALU.subtract),
                    reads=[("u", i), ("w_", i)], writes=[("n1", i)])
                add("pool", lambda e: e.tensor_tensor(out=u, in0=a2, in1=cs, op=ALU.mult),
                    reads=[("a2", i), "cos", ("n1", i)], writes=[("u", i)])
                add("dve", lambda e: e.tensor_tensor(out=w_, in0=a1, in1=sn, op=ALU.mult),
                    reads=[("a1", i), "sin", ("n1", i)], writes=[("w_", i)])
                add("pool", lambda e: e.tensor_tensor(out=n2, in0=u, in1=w_, op=ALU.add),
                    reads=[("u", i), ("w_", i)], writes=[("n2", i)])
                add("sp", lambda e: e.dma_start(out=dst[row0:row0 + 128, tb * 512:(tb + 1) * 512], in_=n1),
                    reads=[("n1", i)], writes=[("qk", isk, row0 // 128)], dma_sem=f"n1{i}")
                add("sp", lambda e: e.dma_start(out=dst[row0 + 128:row0 + 256, tb * 512:(tb + 1) * 512], in_=n2),
                    reads=[("n2", i)], writes=[("qk", isk, row0 // 128 + 1)], dma_sem=f"n2{i}")
            else:
                i = cnt["g"] % 2
                cnt["g"] += 1
                sgb = EE(20480 + i * 1024, [512], BF16)
                b = banks[0]
                row0 = (gi - 8) * 512 + si * 128
                add("act", lambda e: e.activation(out=sgb, in_=pss[b][:], func=AF.Silu),
                    reads=[("ps", b)], writes=[("sgb", i)])
                add("sp", lambda e: e.dma_start(out=sg_d[row0:row0 + 128, tb * 512:(tb + 1) * 512], in_=sgb),
                    reads=[("sgb", i)], writes=[("sgd", row0 // 128)], dma_sem=f"sgb{i}")

        def epi_tok(gi, tt, b):
            i = cnt["v"] % 2
            cnt["v"] += 1
            vb = EE(22528 + i * 1024, [512], BF16)
            h = gi - 16
            eng = "act" if i == 0 else "dve"
            if eng == "act":
                add("act", lambda e: e.activation(out=vb, in_=pss[b][:], func=AF.Copy),
                    reads=[("ps", b)], writes=[("vb", i)])
            else:
                add("dve", lambda e: e.tensor_copy(out=vb, in_=pss[b][:]), reads=[("ps", b)], writes=[("vb", i)])
            add("sp", lambda e: e.dma_start(out=v_d[tt * 128:(tt + 1) * 128, h * 512:(h + 1) * 512], in_=vb),
                reads=[("vb", i)], writes=[("vd", h)], dma_sem=f"vb{i}")

        gemm(Wi, 16, groups, rhs_fn=lambda k, tb, gi: hT[:, k, tb * 512:(tb + 1) * 512], NTB=4, epi=epi,
             lhs_fn=lambda k, tt, gi: hT[:, k, tt * 128:(tt + 1) * 128], NTT=16, epi_tok=epi_tok)

        CUT = os.environ.get("RET_CUT", "")
        if CUT == "inproj":
            return
        S.barrier()
        gam_c = [math.exp(LOGG[h] * 128.0) for h in range(8)]
        for h in range(8):
            s2 = h % 2
            kTh = B.carve(s2 * 56 * 1024 + 0, [2, TOK], BF16)
            vh = B.carve(s2 * 56 * 1024 + 8192, [16, 512], BF16)
            add("sp", lambda e, kTh=kTh, h=h: e.dma_start(
                out=kTh, in_=kT_d.rearrange("(c p) t -> p c t", p=128)[:, 2 * h:2 * h + 2, :]),
                reads=[("qk", True, 2 * h), ("qk", True, 2 * h + 1)], writes=[("A_k", s2)], dma_sem=f"Ak{s2}")
            add("sp", lambda e, vh=vh, h=h: e.dma_start(
                out=vh, in_=v_d.rearrange("(n p) e -> p n e", p=128)[:, :, h * 512:(h + 1) * 512]),
                reads=[("vd", h)], writes=[("A_v", s2)], dma_sem=f"Av{s2}")
            bR = [0, 1]
            for n in range(16):
                bt = 2 + (h * 16 + n) % 6
                ptv = pss[bt][:].bitcast(BF16)
                i = n % 4
                kd = B.carve(SM + 5120 + i * 512, [256], BF16)
                for c in range(2):
                    add("pe", lambda e, c=c, n=n, ptv=ptv, kTh=kTh: e.transpose(
                        ptv[:, c * 128:(c + 1) * 128], kTh[:, c, n * 128:(n + 1) * 128], ident),
                        reads=[("A_k", s2), "cstb"], writes=[("ps", bt)])
                add("act", lambda e, ptv=ptv, kd=kd, n=n, h=h: e.activation(
                    out=kd, in_=ptv[:, 0:256], func=AF.Copy, scale=decA[:, n * 8 + h:n * 8 + h + 1]),
                    reads=[("ps", bt), "cst"], writes=[("A_kd", i)])
                for c in range(2):
                    add("pe", lambda e, c=c, n=n, kd=kd, vh=vh: e.matmul(
                        pss[bR[c]][:], lhsT=kd[:, c * 128:(c + 1) * 128], rhs=vh[:, n, :], start=(n == 0),
                        stop=(n == 15)),
                        reads=[("A_kd", i), ("A_v", s2)], writes=[("ps", bR[c])])
            for c in range(2):
                i = (2 * h + c) % 2
                rst = B.carve(124 * 1024 + i * 2048, [512], F32)
                add("dve", lambda e, c=c, rst=rst: e.tensor_copy(out=rst, in_=pss[bR[c]][:]),
                    reads=[("ps", bR[c])], writes=[("A_rst", i)])
                r0 = (h * 2 + c) * 128
                add("sp", lambda e, rst=rst, r0=r0: e.dma_start(out=rfin_d[r0:r0 + 128, :], in_=rst),
                    reads=[("A_rst", i)], writes=["rfin"], dma_sem=f"Ar{i}")
        if not os.environ.get("NOCC"):
            add("pool", lambda e: e.collective_compute("AllGather", ALU.bypass,
                                                    replica_groups=[list(range(8))],
                                                    ins=[rfin_d], outs=[rall_d]),
                reads=["rfin"], writes=["rall"], dma_sem="cc", inc=1)
        S.barrier()
        if CUT == "phaseA":
            return
        Rf = B.carve(128 * 1024, [16, 512], F32)
        Rb = [B.carve(160 * 1024 + p * 2048, [2, 512], BF16) for p in range(2)]
        cnt_ld = 0
        for h in range(8):
            for c in range(2):
                dst = Rf[:, h * 2 + c, :]
                r0 = (h * 2 + c) * 128
                for half in range(2):
                    i = cnt_ld % 2
                    cnt_ld += 1
                    ld = B.carve(112 * 1024 + i * 8192, [3, 512], F32)
                    add("sp", lambda e, ld=ld, r0=r0, half=half: e.dma_start(
                        out=ld, in_=rall_d.rearrange("(s r) e -> r s e", s=8)[r0:r0 + 128, 4 * half:4 * half + 3, :]),
                        reads=["rall"], writes=[("B_ld", i)], dma_sem=f"Bl{i}")
                    for s3 in range(3):
                        sl_ = 4 * half + s3
                        if half == 0 and s3 == 0:
                            add("dve", lambda e, ld=ld, dst=dst, h=h: e.tensor_scalar(
                                out=dst, in0=ld[:, 0, :], scalar1=coefR[:, h:h + 1], scalar2=None, op0=ALU.mult),
                                reads=[("B_ld", i), "cst"], writes=[("Rf", h, c)])
                        else:
                            add("dve", lambda e, ld=ld, dst=dst, h=h, s3=s3, sl_=sl_: e.scalar_tensor_tensor(
                                out=dst, in0=ld[:, s3, :], scalar=coefR[:, sl_ * 8 + h:sl_ * 8 + h + 1], in1=dst,
                                op0=ALU.mult, op1=ALU.add),
                                reads=[("B_ld", i), "cst"], writes=[("Rf", h, c)])
        if CUT == "rin":
            return
        for h in range(8):
            s2 = h % 2
            base = s2 * 56 * 1024
            qTh = B.carve(base, [2, TOK], BF16)
            kTh = B.carve(base + 8192, [2, TOK], BF16)
            vh = B.carve(base + 16384, [16, 512], BF16)
            sgh = B.carve(base + 32768, [4, TOK], BF16)
            qdh = B.carve(base + 49152, [2, TOK], BF16)
            yst = B.carve(164 * 1024, [4, TOK], BF16)
            add("sp", lambda e, qTh=qTh, h=h: e.dma_start(
                out=qTh, in_=qT_d.rearrange("(c p) t -> p c t", p=128)[:, 2 * h:2 * h + 2, :]),
                reads=[("qk", False, 2 * h), ("qk", False, 2 * h + 1)], writes=[("S_q", s2)], dma_sem=f"Sq{s2}")
            add("sp", lambda e, kTh=kTh, h=h: e.dma_start(
                out=kTh, in_=kT_d.rearrange("(c p) t -> p c t", p=128)[:, 2 * h:2 * h + 2, :]),
                reads=[("qk", True, 2 * h), ("qk", True, 2 * h + 1)], writes=[("S_k", s2)], dma_sem=f"Sk{s2}")
            add("sp", lambda e, vh=vh, h=h: e.dma_start(
                out=vh, in_=v_d.rearrange("(n p) e -> p n e", p=128)[:, :, h * 512:(h + 1) * 512]),
                reads=[("vd", h)], writes=[("S_v", s2)], dma_sem=f"Sv{s2}")
            add("sp", lambda e, sgh=sgh, h=h: e.dma_start(
                out=sgh, in_=sg_d.rearrange("(c p) t -> p c t", p=128)[:, 4 * h:4 * h + 4, :]),
                reads=[("sgd", 4 * h + j) for j in range(4)], writes=[("S_g", s2)], dma_sem=f"Sg{s2}")
            for c in range(2):
                add("pool", lambda e, c=c, qdh=qdh, qTh=qTh, h=h: e.tensor_tensor(
                    out=qdh[:, c, :].rearrange("p (n i) -> p n i", i=128),
                    in0=qTh[:, c, :].rearrange("p (n i) -> p n i", i=128),
                    in1=qdec[:, h, :].unsqueeze(1).to_broadcast([128, 16, 128]), op=ALU.mult),
                    reads=[("S_q", s2), "cst"], writes=[("S_qd", s2, c)])
            for c in range(2):
                add("act", lambda e, c=c, h=h: e.activation(out=Rb[0][:, c, :], in_=Rf[:, h * 2 + c, :],
                                                             func=AF.Copy),
                    reads=[("Rf", h, c)], writes=[("Rb", 0, c)])
            for n in range(16):
                par = n % 2
                sl = slice(n * 128, (n + 1) * 128)
                bS = B.nb()
                for c in range(2):
                    add("pe", lambda e, c=c, bS=bS, kTh=kTh, qTh=qTh, sl=sl: e.matmul(
                        pss[bS][:, 0:128], lhsT=kTh[:, c, sl], rhs=qTh[:, c, sl], start=(c == 0), stop=(c == 1)),
                        reads=[("S_k", s2), ("S_q", s2)], writes=[("ps", bS)])
                STs = B.carve(SM + par * 256, [128], BF16)
                add("dve", lambda e, bS=bS, STs=STs, h=h: e.tensor_tensor(
                    out=STs, in0=pss[bS][:, 0:128], in1=dmaskT[:, h, :], op=ALU.mult),
                    reads=[("ps", bS), "cst"], writes=[("STs", par)])
                bt = B.nb()
                ptv = pss[bt][:].bitcast(BF16)
                kd = B.carve(SM + 512 + par * 512, [256], BF16)
                for c in range(2):
                    add("pe", lambda e, c=c, ptv=ptv, kTh=kTh, sl=sl: e.transpose(
                        ptv[:, c * 128:(c + 1) * 128], kTh[:, c, sl], ident),
                        reads=[("S_k", s2), "cstb"], writes=[("ps", bt)])
                add("act", lambda e, ptv=ptv, kd=kd, h=h: e.activation(
                    out=kd, in_=ptv[:, 0:256], func=AF.Copy, scale=kdec[:, h:h + 1]),
                    reads=[("ps", bt), "cst"], writes=[("kd", par)])
                bO = B.nb()
                add("pe", lambda e, bO=bO, STs=STs, vh=vh, n=n: e.matmul(
                    pss[bO][:], lhsT=STs, rhs=vh[:, n, :], start=True, stop=False),
                    reads=[("STs", par), ("S_v", s2)], writes=[("ps", bO)])
                for c in range(2):
                    add("pe", lambda e, c=c, bO=bO, qdh=qdh, sl=sl, par=par: e.matmul(
                        pss[bO][:], lhsT=qdh[:, c, sl], rhs=Rb[par][:, c, :], start=False, stop=(c == 1)),
                        reads=[("S_qd", s2, c), ("Rb", par, c)], writes=[("ps", bO)])
                if n < 15:
                    for c in range(2):
                        bU = B.nb()
                        add("pe", lambda e, c=c, bU=bU, kd=kd, vh=vh, n=n: e.matmul(
                            pss[bU][:], lhsT=kd[:, c * 128:(c + 1) * 128], rhs=vh[:, n, :], start=True, stop=True),
                            reads=[("kd", par), ("S_v", s2)], writes=[("ps", bU)])
                        dst = Rf[:, h * 2 + c, :]
                        add("dve", lambda e, bU=bU, dst=dst, h=h: e.scalar_tensor_tensor(
                            out=dst, in0=dst, scalar=float(gam_c[h]), in1=pss[bU][:], op0=ALU.mult, op1=ALU.add),
                            reads=[("ps", bU), ("Rf", h, c)], writes=[("Rf", h, c)])
                        add("act", lambda e, c=c, dst=dst, par=par: e.activation(
                            out=Rb[1 - par][:, c, :], in_=dst, func=AF.Copy),
                            reads=[("Rf", h, c)], writes=[("Rb", 1 - par, c)])
                junk = B.carve(SM + 1536, [512], BF16)
                ss = B.carve(SM + 2560 + par * 16, [1], F32)
                add("act", lambda e, bO=bO, junk=junk, ss=ss: e.activation(
                    out=junk, in_=pss[bO][:], func=AF.Square, accum_out=ss),
                    reads=[("ps", bO)], writes=[("ss", par), "junk"])
                add("act", lambda e, ss=ss: e.activation(out=ss, in_=ss, func=AF.Sqrt, bias=epsc[:, 0:1],
                                                          scale=1.0 / 512.0),
                    reads=[("ss", par), "cst"], writes=[("ss", par)])
                add("dve", lambda e, ss=ss: e.reciprocal(out=ss, in_=ss), reads=[("ss", par)], writes=[("ss", par)])
                yn = B.carve(SM + 2624 + par * 1024, [512], BF16)
                add("act", lambda e, bO=bO, yn=yn, ss=ss: e.activation(out=yn, in_=pss[bO][:], func=AF.Copy,
                                                                        scale=ss[:, 0:1]),
                    reads=[("ps", bO), ("ss", par)], writes=[("yn", par)])
                bY = B.nb()
                pyv = pss[bY][:].bitcast(BF16)
                for ec in range(4):
                    add("pe", lambda e, ec=ec, pyv=pyv, yn=yn: e.transpose(
                        pyv[:, ec * 128:(ec + 1) * 128], yn[:, ec * 128:(ec + 1) * 128], ident),
                        reads=[("yn", par), "cstb"], writes=[("ps", bY)])
                add("dve", lambda e, pyv=pyv, yst=yst, sgh=sgh, sl=sl: e.tensor_tensor(
                    out=yst[:, :, sl], in0=pyv[:, 0:512].rearrange("p (c i) -> p c i", c=4), in1=sgh[:, :, sl],
                    op=ALU.mult),
                    reads=[("ps", bY), ("S_g", s2)], writes=["yst"])
            add("sp", lambda e, yst=yst, h=h: e.dma_start(
                out=yT_d.rearrange("(c p) t -> p c t", p=128)[:, 4 * h:4 * h + 4, :], in_=yst),
                reads=["yst"], writes=[("yT", 4 * h + j) for j in range(4)], dma_sem="yst")
        S.barrier()
        if CUT == "scan":
            return
        Wo = WFULL[('w_ret_out', l)]
        for ps_ in range(2):
            t0 = ps_ * 1024
            yT = IN([32, 1024], BF16)
            for q4 in range(4):
                add("sp", lambda e, q4=q4, t0=t0, yT=yT: e.dma_start(
                    out=yT[:, q4 * 8:(q4 + 1) * 8, :],
                    in_=yT_d.rearrange("(c p) t -> p c t", p=128)[:, q4 * 8:(q4 + 1) * 8, t0:t0 + 1024]),
                    reads=[("yT", c) for c in range(q4 * 8, (q4 + 1) * 8)],
                    writes=[("in", c) for c in range(44)], dma_sem="inl")
            groups2 = [dict(cols=[(j * 512, 512)], sets=[[0], [1], [2], [3]]) for j in range(4)]
            gemm(Wo, 32, groups2, rhs_fn=lambda k, tb, gi, yT=yT: yT[:, k, tb * 512:(tb + 1) * 512], NTB=2,
                 epi=make_resid_epi(lambda gi, si: gi * 4 + si, t0))
        return False


    LG = [TOK // d for d in DIL]
    SEG = [128 + L for L in LG]
    VB0 = [0, 17, 37]

    def perm_rhs(hT, k, tb, g):
        if g == 0:
            return hT[:, k, tb * 512:(tb + 1) * 512]
        if g == 1:
            return hT[:, k, tb:TOK:4]
        return hT[:, k, :].rearrange("p (l r) -> p r l", r=16)[:, 4 * tb:4 * tb + 4, :]

    def perm_lhs(hT, k, tt, g):
        if g == 0:
            return hT[:, k, tt * 128:(tt + 1) * 128]
        if g == 1:
            r, q = tt // 4, tt % 4
            return hT[:, k, q * 512 + r:q * 512 + r + 509:4]
        return hT[:, k, tt:TOK:16]

    def kv_phase():
        hT = norm_phase(8)
        Wk = WFULL[("w_kv", 0)]
        groups = []
        for g in range(3):
            for j in range(4):
                groups.append(dict(cols=[(g * 2048 + j * 512, 512)], sets=[[0], [1], [2], [3]]))
        for g in range(3):
            for j in range(4):
                groups.append(dict(cols=[(6144 + g * 2048 + j * 512, 512)], tok=True))
        cnt = {"k": 0, "v": 0}

        def epi(gi, si, tb, banks):
            g, j = gi // 4, gi % 4
            i = cnt["k"] % 2
            cnt["k"] += 1
            st_ = EE(i * 1024, [512], BF16)
            b = banks[0]
            if i == 0:
                add("act", lambda e: e.activation(out=st_, in_=pss[b][:], func=AF.Copy),
                    reads=[("ps", b)], writes=[("kst", i)])
            else:
                add("dve", lambda e: e.tensor_copy(out=st_, in_=pss[b][:]), reads=[("ps", b)], writes=[("kst", i)])
            r0 = (4 * j + si) * 128
            rows = slice(r0, r0 + 128)
            if g == 0:
                dst = Kc_d[0][rows, 128 + tb * 512:128 + (tb + 1) * 512]
                src = st_
            elif g == 1:
                dst = Kc_d[1][rows, tb * 640 + 128:tb * 640 + 640]
                src = st_
            else:
                dst = Kc_d[2][rows, :].rearrange("p (r x) -> p r x", x=256)[:, 4 * tb:4 * tb + 4, 128:256]
                src = st_.rearrange("p (r l) -> p r l", r=4)
            add("sp", lambda e: e.dma_start(out=dst, in_=src), reads=[("kst", i)], writes=[("Kc", g)],
                dma_sem=f"kst{i}")
            if g == 0 and tb == 3:
                add("sp", lambda e: e.dma_start(out=KH_d[rows, 0:128], in_=st_[:, 384:512]),
                    reads=[("kst", i)], writes=["KH"], dma_sem=f"kst{i}")
            elif g == 1:
                add("sp", lambda e: e.dma_start(out=KH_d[rows, 128 + tb * 128:256 + tb * 128], in_=st_[:, 384:512]),
                    reads=[("kst", i)], writes=["KH"], dma_sem=f"kst{i}")
            elif g == 2:
                add("sp", lambda e: e.dma_start(out=KH_d[rows, 640 + tb * 512:640 + (tb + 1) * 512], in_=st_),
                    reads=[("kst", i)], writes=["KH"], dma_sem=f"kst{i}")

        def epi_tok(gi, tt, b):
            g, j = (gi - 12) // 4, (gi - 12) % 4
            i = cnt["v"] % 2
            cnt["v"] += 1
            st_ = EE(2048 + i * 1024, [512], BF16)
            if i == 0:
                add("act", lambda e: e.activation(out=st_, in_=pss[b][:], func=AF.Copy),
                    reads=[("ps", b)], writes=[("vst", i)])
            else:
                add("dve", lambda e: e.tensor_copy(out=st_, in_=pss[b][:]), reads=[("ps", b)], writes=[("vst", i)])
            cols = slice(j * 512, (j + 1) * 512)
            if g == 0:
                r0 = 128 + tt * 128
            elif g == 1:
                r0 = (tt // 4) * 640 + 128 + (tt % 4) * 128
            else:
                r0 = tt * 256 + 128
            add("sp", lambda e: e.dma_start(out=Vc_d[g][r0:r0 + 128, cols], in_=st_), reads=[("vst", i)],
                writes=[("Vc", g)], dma_sem=f"vst{i}")
            h0 = None
            if g == 0 and tt == 15:
                h0 = 0
            elif g == 1 and tt % 4 == 3:
                h0 = 128 + (tt // 4) * 128
            elif g == 2:
                h0 = 640 + tt * 128
            if h0 is not None:
                add("sp", lambda e: e.dma_start(out=VH_d[h0:h0 + 128, cols], in_=st_), reads=[("vst", i)],
                    writes=["VH"], dma_sem=f"vst{i}")

        gemm(Wk, 16, groups, rhs_fn=lambda k, tb, gi: perm_rhs(hT, k, tb, gi // 4), NTB=4, epi=epi,
             lhs_fn=lambda k, tt, gi: perm_lhs(hT, k, tt, (gi - 12) // 4), NTT=16, epi_tok=epi_tok)
        for p_ in range(4):
            add("pool", lambda e, p_=p_: e.collective_compute(
                "AllGather", ALU.bypass, replica_groups=[list(range(8))],
                ins=[KH_d[p_ * 512:(p_ + 1) * 512, :]], outs=[KHall_p[p_]]),
                reads=["KH"], writes=["KHall"], dma_sem="cch", inc=1)
        for p_ in range(4):
            add("pool", lambda e, p_=p_: e.collective_compute(
                "AllGather", ALU.bypass, replica_groups=[list(range(8))],
                ins=[VH_d[VPART[p_]:VPART[p_ + 1], :]], outs=[VHall_p[p_]]),
                reads=["VH"], writes=["VHall"], dma_sem="cch", inc=1)
        S.barrier()
        halo_select()

    def halo_select():
        selI = B.carve(SM, [8, 128], BF16)
        for s_ in range(8):
            add("dve", lambda e, s_=s_: e.tensor_scalar(out=selI[:, s_, :], in0=ident, scalar1=maskH[:, s_:s_ + 1],
                                                        scalar2=None, op0=ALU.mult),
                reads=["cst", "cstb"], writes=["selI"])
        SL = (0, 1, 2, 4, 5, 6)
        hblocks = [(0, 0, 128, 0)] + [(128, 1, 512, 1)] + [(640 + q * 512, 2, 512, q) for q in range(4)]
        n_ = 0
        for fc in range(16):
            for (c0, g, w, q) in hblocks:
                i = n_ % 6
                n_ += 1
                ld = B.carve(i * 8192, [6, 512], BF16)
                for half in range(2):
                    add("sp", lambda e, ld=ld, half=half, c0=c0, w=w, fc=fc: e.dma_start(
                        out=ld[:, 3 * half:3 * half + 3, 0:w],
                        in_=KHall_p[fc // 4].rearrange("(s r) c -> r s c", s=8)[(fc % 4) * 128:(fc % 4 + 1) * 128,
                                                                              4 * half:4 * half + 3, c0:c0 + w]),
                        reads=["KHall"], writes=[("hld", i)], dma_sem=f"hld{i}")
                b = B.nb()
                for t_, s_ in enumerate(SL):
                    add("pe", lambda e, t_=t_, s_=s_, b=b, ld=ld, w=w: e.matmul(
                        pss[b][:, 0:w], lhsT=selI[:, s_, :], rhs=ld[:, t_, 0:w], start=(t_ == 0), stop=(t_ == 5)),
                        reads=[("hld", i), "selI"], writes=[("ps", b)])
                st_ = B.carve(49152 + i * 1024, [512], BF16)
                add("act", lambda e, b=b, st_=st_, w=w: e.activation(out=st_[:, 0:w], in_=pss[b][:, 0:w], func=AF.Copy),
                    reads=[("ps", b)], writes=[("hst", i)])
                rows = slice(fc * 128, (fc + 1) * 128)
                if g == 0:
                    dst, src = Kc_d[0][rows, 0:128], st_[:, 0:128]
                elif g == 1:
                    dst = Kc_d[1][rows, :].rearrange("p (r x) -> p r x", x=640)[:, :, 0:128]
                    src = st_.rearrange("p (r l) -> p r l", r=4)
                else:
                    dst = Kc_d[2][rows, :].rearrange("p (r x) -> p r x", x=256)[:, 4 * q:4 * q + 4, 0:128]
                    src = st_.rearrange("p (r l) -> p r l", r=4)
                add("sp", lambda e, dst=dst, src=src: e.dma_start(out=dst, in_=src), reads=[("hst", i)],
                    writes=[("Kc", g)], dma_sem=f"hst{i}")
        for hb in range(21):
            if hb == 0:
                g, r = 0, 0
            elif hb < 5:
                g, r = 1, hb - 1
            else:
                g, r = 2, hb - 5
            for cb in range(4):
                i = n_ % 6
                n_ += 1
                ld = B.carve(i * 8192, [6, 512], BF16)
                vp = max(p_ for p_ in range(4) if VPART[p_] <= hb * 128)
                vr = hb * 128 - VPART[vp]
                for half in range(2):
                    add("sp", lambda e, ld=ld, half=half, vp=vp, vr=vr, cb=cb: e.dma_start(
                        out=ld[:, 3 * half:3 * half + 3, :],
                        in_=VHall_p[vp].rearrange("(s r) c -> r s c", s=8)[vr:vr + 128,
                                                                          4 * half:4 * half + 3, cb * 512:(cb + 1) * 512]),
                        reads=["VHall"], writes=[("hld", i)], dma_sem=f"hld{i}")
                b = B.nb()
                for t_, s_ in enumerate(SL):
                    add("pe", lambda e, t_=t_, s_=s_, b=b, ld=ld: e.matmul(
                        pss[b][:], lhsT=selI[:, s_, :], rhs=ld[:, t_, :], start=(t_ == 0), stop=(t_ == 5)),
                        reads=[("hld", i), "selI"], writes=[("ps", b)])
                st_ = B.carve(49152 + i * 1024, [512], BF16)
                if i % 2 == 0:
                    add("act", lambda e, b=b, st_=st_: e.activation(out=st_, in_=pss[b][:], func=AF.Copy),
                        reads=[("ps", b)], writes=[("hst", i)])
                else:
                    add("dve", lambda e, b=b, st_=st_: e.tensor_copy(out=st_, in_=pss[b][:]),
                        reads=[("ps", b)], writes=[("hst", i)])
                r0 = r * SEG[g]
                add("sp", lambda e, st_=st_, g=g, r0=r0, cb=cb: e.dma_start(
                    out=Vc_d[g][r0:r0 + 128, cb * 512:(cb + 1) * 512], in_=st_),
                    reads=[("hst", i)], writes=[("Vc", g)], dma_sem=f"hst{i}")
        S.barrier()

    def lse_views(dram2d):
        v0 = dram2d.rearrange("(u i) h -> i u h", i=128)
        v1 = dram2d.rearrange("(n i r) h -> i r n h", n=4, i=128, r=4)
        v2 = dram2d.rearrange("(i r) h -> i r h", r=16)
        return [v0, v1, v2]

    def att_layer(l):
        j = l - 2
        hT = norm_phase(l)
        Wq = WFULL[("w_att_q", j)]
        groups = []
        for g in range(3):
            for jj in range(4):
                groups.append(dict(cols=[(g * 2048 + jj * 512, 512)], sets=[[0], [1], [2], [3]]))
        cnt = {"q": 0}

        def epi(gi, si, tb, banks):
            g, jj = gi // 4, gi % 4
            i = cnt["q"] % 2
            cnt["q"] += 1
            st_ = EE(i * 1024, [512], BF16)
            b = banks[0]
            add("act", lambda e: e.activation(out=st_, in_=pss[b][:], func=AF.Copy, scale=128.0 ** -0.5),
                reads=[("ps", b)], writes=[("qst", i)])
            r0 = (4 * jj + si) * 128
            add("sp", lambda e: e.dma_start(out=qa_d[g][r0:r0 + 128, tb * 512:(tb + 1) * 512], in_=st_),
                reads=[("qst", i)], writes=[("qa", g)], dma_sem=f"qst{i}")

        gemm(Wq, 16, groups, rhs_fn=lambda k, tb, gi: perm_rhs(hT, k, tb, gi // 4), NTB=4, epi=epi)
        S.barrier()
        HS = 53 * 1024
        NBU = 6
        lseall = B.carve(130 * 1024, [48, 16], F32)
        negL = B.carve(134 * 1024, [48, 16], F32)

        def head_bufs(h):
            base = (h % 2) * HS
            qh = B.carve(base, [3, TOK], BF16)
            Kh = [B.carve(base + 12288 + 2 * sum(NCG[:g]), [NCG[g]], BF16) for g in range(3)]
            bia = B.carve(base + 12288 + 17664, [6, 256], F32)
            Vh = B.carve(base + 12288 + 17664 + 6144, [69, 128], BF16)
            return qh, Kh, bia, Vh

        def load_head(h, with_v):
            qh, Kh, bia, Vh = head_bufs(h)
            s2 = h % 2
            for g in range(3):
                add("sp", lambda e, g=g, qh=qh: e.dma_start(out=qh[:, g, :], in_=qa_d[g][h * 128:(h + 1) * 128, :]),
                    reads=[("qa", g)], writes=[("hq", s2)], dma_sem=f"hq{s2}")
                add("sp", lambda e, g=g, Kh=Kh: e.dma_start(out=Kh[g], in_=Kc_d[g][h * 128:(h + 1) * 128, :]),
                    reads=[("Kc", g)], writes=[("hk", s2)], dma_sem=f"hq{s2}")
            add("sp", lambda e, bia=bia: e.dma_start(out=bia, in_=bias_d[h].rearrange("p (a c) -> p a c", a=6)),
                writes=[("hb", s2)], dma_sem=f"hq{s2}")
            if with_v:
                for g in range(3):
                    nb_ = NCG[g] // 128
                    add("sp", lambda e, g=g, Vh=Vh, nb_=nb_: e.dma_start(
                        out=Vh[:, VB0[g]:VB0[g] + nb_, :],
                        in_=Vc_d[g].rearrange("(b c) e -> c b e", c=128)[:, :, h * 128:(h + 1) * 128]),
                        reads=[("Vc", g)], writes=[("hv", s2)], dma_sem=f"hv{s2}")
            return qh, Kh, bia, Vh

        def unit_geo(g, u):
            if g == 0:
                r, nloc = 0, u
            elif g == 1:
                r, nloc = u // 4, u % 4
            else:
                r, nloc = u, 0
            kc = r * SEG[g] + nloc * 128
            return kc, (1 if nloc == 0 else 0)

        for h in range(16):
            s2 = h % 2
            qh, Kh, bia, Vh = load_head(h, False)
            negm = B.carve(166 * 1024 + s2 * 512, [48], F32)
            den = B.carve(167 * 1024 + s2 * 512, [48], F32)
            for g in range(3):
                for u in range(16):
                    uu = g * 16 + u
                    par = uu % NBU
                    kc, var = unit_geo(g, u)
                    b = B.nb()
                    add("pe", lambda e, b=b, g=g, u=u, kc=kc, qh=qh, Kh=Kh: e.matmul(
                        pss[b][:, 0:256], lhsT=qh[:, g, u * 128:(u + 1) * 128], rhs=Kh[g][:, kc:kc + 256],
                        start=True, stop=True),
                        reads=[("hq", s2), ("hk", s2)], writes=[("ps", b)])
                    sb_ = B.carve(150 * 1024 + par * 1024, [256], F32)
                    add("dve", lambda e, b=b, sb_=sb_, g=g, var=var, bia=bia: e.tensor_tensor(
                        out=sb_, in0=pss[b][:, 0:256], in1=bia[:, g * 2 + var, :], op=ALU.add),
                        reads=[("ps", b), ("hb", s2)], writes=[("sb", par)])
                    add("dve", lambda e, sb_=sb_, negm=negm, uu=uu: e.tensor_reduce(
                        out=negm[:, uu:uu + 1], in_=sb_, axis=AX.X, op=ALU.max, negate=True),
                        reads=[("sb", par)], writes=[("negm", s2)])
                    junk = B.carve(156 * 1024 + par * 1024, [256], F32)
                    add("act", lambda e, sb_=sb_, junk=junk, negm=negm, den=den, uu=uu: e.activation(
                        out=junk, in_=sb_, func=AF.Exp, bias=negm[:, uu:uu + 1], scale=1.0,
                        accum_out=den[:, uu:uu + 1]),
                        reads=[("sb", par), ("negm", s2)], writes=[("den", s2), ("junk", par)])
            add("act", lambda e, den=den: e.activation(out=den, in_=den, func=AF.Ln),
                reads=[("den", s2)], writes=[("den", s2)])
            add("dve", lambda e, den=den, negm=negm, h=h: e.tensor_tensor(
                out=lseall[:, :, h], in0=den, in1=negm, op=ALU.subtract),
                reads=[("den", s2), ("negm", s2)], writes=["lseall"])
        for g in range(3):
            src = lseall[:, g * 16:(g + 1) * 16, :]
            if g == 1:
                src = src.rearrange("p (r n) h -> p r n h", r=4)
            add("sp", lambda e, g=g, src=src: e.dma_start(out=lse_views(lse_d[g])[g], in_=src),
                reads=["lseall"], writes=["lse_d"], dma_sem="lsx")
        nat = B.carve(138 * 1024, [16, 48], F32)
        natv = nat.rearrange("p t (g h) -> p t g h", g=3)
        for g in range(3):
            add("sp", lambda e, g=g: e.dma_start(out=natv[:, :, g, :],
                                                 in_=lse_d[g].rearrange("(t p) h -> p t h", p=128)),
                reads=["lse_d"], writes=["nat"], dma_sem="lsx")
        Mx = B.carve(141 * 1024, [16, 16], F32)
        ex = B.carve(142 * 1024, [16, 48], F32)
        exv = ex.rearrange("p t (g h) -> p t g h", g=3)
        sm_ = B.carve(145 * 1024, [16, 16], F32)
        add("dve", lambda e: e.tensor_tensor(out=Mx, in0=natv[:, :, 0, :], in1=natv[:, :, 1, :], op=ALU.max),
            reads=["nat"], writes=["Mx"])
        add("dve", lambda e: e.tensor_tensor(out=Mx, in0=Mx, in1=natv[:, :, 2, :], op=ALU.max),
            reads=["nat", "Mx"], writes=["Mx"])
        for g in range(3):
            add("dve", lambda e, g=g: e.tensor_tensor(out=exv[:, :, g, :], in0=natv[:, :, g, :], in1=Mx,
                                                       op=ALU.subtract),
                reads=["nat", "Mx"], writes=["ex"])
        add("act", lambda e: e.activation(out=ex, in_=ex, func=AF.Exp), reads=["ex"], writes=["ex"])
        add("dve", lambda e: e.tensor_tensor(out=sm_, in0=exv[:, :, 0, :], in1=exv[:, :, 1, :], op=ALU.add),
            reads=["ex"], writes=["sm"])
        add("dve", lambda e: e.tensor_tensor(out=sm_, in0=sm_, in1=exv[:, :, 2, :], op=ALU.add),
            reads=["ex", "sm"], writes=["sm"])
        add("act", lambda e: e.activation(out=sm_, in_=sm_, func=AF.Ln), reads=["sm"], writes=["sm"])
        add("dve", lambda e: e.scalar_tensor_tensor(out=sm_, in0=sm_, scalar=-1.0, in1=Mx, op0=ALU.mult,
                                                     op1=ALU.subtract),
            reads=["sm", "Mx"], writes=["sm"])
        add("sp", lambda e: e.dma_start(out=LSE_d.rearrange("(t p) h -> p t h", p=128), in_=sm_),
            reads=["sm"], writes=["LSE_d"], dma_sem="lsx")
        for g in range(3):
            dst = negL[:, g * 16:(g + 1) * 16, :]
            if g == 1:
                dst = dst.rearrange("p (r n) h -> p r n h", r=4)
            add("sp", lambda e, g=g, dst=dst: e.dma_start(out=dst, in_=lse_views(LSE_d)[g]),
                reads=["LSE_d"], writes=["negL"], dma_sem="lsx")
        for h in range(16):
            s2 = h % 2
            qh, Kh, bia, Vh = load_head(h, True)
            oacc = B.carve(110 * 1024 + s2 * 8192, [TOK], F32)
            for g in range(3):
                for u in range(16):
                    uu = g * 16 + u
                    par = uu % NBU
                    kc, var = unit_geo(g, u)
                    kb = VB0[g] + kc // 128
                    b = uu % 3
                    add("pe", lambda e, b=b, g=g, u=u, kc=kc, qh=qh, Kh=Kh: e.matmul(
                        pss[b][:, 0:256], lhsT=qh[:, g, u * 128:(u + 1) * 128], rhs=Kh[g][:, kc:kc + 256],
                        start=True, stop=True),
                        reads=[("hq", s2), ("hk", s2)], writes=[("ps", b)])
                    sb_ = B.carve(150 * 1024 + par * 1024, [256], F32)
                    add("dve", lambda e, b=b, sb_=sb_, g=g, var=var, bia=bia: e.tensor_tensor(
                        out=sb_, in0=pss[b][:, 0:256], in1=bia[:, g * 2 + var, :], op=ALU.add),
                        reads=[("ps", b), ("hb", s2)], writes=[("sb", par)])
                    P_ = B.carve(156 * 1024 + par * 1024, [256], BF16)
                    add("act", lambda e, sb_=sb_, P_=P_, uu=uu, h=h: e.activation(
                        out=P_, in_=sb_, func=AF.Exp, bias=negL[:, uu, h:h + 1], scale=1.0),
                        reads=[("sb", par), "negL"], writes=[("P", par)])
                    bt = 3 + uu % 3
                    ptv = pss[bt][:].bitcast(BF16)
                    for hf in range(2):
                        add("pe", lambda e, hf=hf, ptv=ptv, P_=P_: e.transpose(
                            ptv[:, hf * 128:(hf + 1) * 128], P_[:, hf * 128:(hf + 1) * 128], ident),
                            reads=[("P", par), "cstb"], writes=[("ps", bt)])
                    PT = B.carve(162 * 1024 + par * 512, [256], BF16)
                    if uu % 2 == 0:
                        add("act", lambda e, ptv=ptv, PT=PT: e.activation(out=PT, in_=ptv[:, 0:256], func=AF.Copy),
                            reads=[("ps", bt)], writes=[("PT", par)])
                    else:
                        add("dve", lambda e, ptv=ptv, PT=PT: e.tensor_copy(out=PT, in_=ptv[:, 0:256]),
                            reads=[("ps", bt)], writes=[("PT", par)])
                    bo = 6 + (uu // 4) % 2
                    cs = (u % 4) * 128
                    for hf in range(2):
                        add("pe", lambda e, hf=hf, bo=bo, cs=cs, kb=kb, Vh=Vh, PT=PT: e.matmul(
                            pss[bo][:, cs:cs + 128], lhsT=Vh[:, kb + hf, :], rhs=PT[:, hf * 128:(hf + 1) * 128],
                            start=(hf == 0), stop=(hf == 1)),
                            reads=[("PT", par), ("hv", s2)], writes=[("ps", bo)])
                    if u % 4 == 3:
                        q = u // 4
                        if g == 0:
                            add("dve", lambda e, bo=bo, q=q, oacc=oacc: e.tensor_copy(
                                out=oacc[:, q * 512:(q + 1) * 512], in_=pss[bo][:]),
                                reads=[("ps", bo)], writes=[("oacc", s2)])
                        elif g == 1:
                            ov = oacc[:, q:TOK:4]
                            add("dve", lambda e, bo=bo, ov=ov: e.tensor_tensor(out=ov, in0=pss[bo][:], in1=ov,
                                                                              op=ALU.add),
                                reads=[("ps", bo), ("oacc", s2)], writes=[("oacc", s2)])
                        else:
                            ov = oacc.rearrange("p (i r) -> p r i", r=16)[:, 4 * q:4 * q + 4, :]
                            add("dve", lambda e, bo=bo, ov=ov: e.tensor_tensor(
                                out=ov, in0=pss[bo][:].rearrange("p (r i) -> p r i", r=4), in1=ov, op=ALU.add),
                                reads=[("ps", bo), ("oacc", s2)], writes=[("oacc", s2)])
            ost = B.carve(126 * 1024, [TOK], BF16)
            add("act", lambda e, oacc=oacc, ost=ost: e.activation(out=ost, in_=oacc, func=AF.Copy),
                reads=[("oacc", s2)], writes=["ost"])
            add("sp", lambda e, ost=ost, h=h: e.dma_start(out=oT_d[h * 128:(h + 1) * 128, :], in_=ost),
                reads=["ost"], writes=[("oT", h)], dma_sem="ost")
        S.barrier()
        oT = IN([16, TOK], BF16)
        for q4 in range(4):
            add("sp", lambda e, q4=q4: e.dma_start(
                out=oT[:, q4 * 4:(q4 + 1) * 4, :],
                in_=oT_d.rearrange("(c p) t -> p c t", p=128)[:, q4 * 4:(q4 + 1) * 4, :]),
                reads=[("oT", c) for c in range(q4 * 4, (q4 + 1) * 4)],
                writes=[("in", c) for c in range(44)], dma_sem="inl")
        Wo = WFULL[("w_att_out", j)]
        groups2 = [dict(cols=[(jj * 512, 512)], sets=[[0], [1], [2], [3]]) for jj in range(4)]
        gemm(Wo, 16, groups2, rhs_fn=lambda k, tb, gi: oT[:, k, tb * 512:(tb + 1) * 512], NTB=4,
             epi=make_resid_epi(lambda gi, si: gi * 4 + si, 0))

    stages = B.stop_after if B.stop_after is not None else ALL_STAGES
    for stg in stages:
        if stg == "kv":
            gather_weight("w_kv", 0)
            continue
        l = int(stg[3:])
        if stg.startswith("ret"):
            gather_weight("w_ret_in", l)
            gather_weight("w_ret_out", l)
        elif stg.startswith("ffn"):
            gather_weight("w_ffn_in", l)
            gather_weight("w_ffn_out", l)
        elif stg.startswith("att"):
            gather_weight("w_att_q", l - 2)
            gather_weight("w_att_out", l - 2)
    B.wnames = list(WFULL.keys())
    for stg in stages:
        if stg.startswith("ret"):
            ret_layer(int(stg[3:]))
        elif stg.startswith("ffn"):
            ffn_layer(int(stg[3:]))
        elif stg == "kv":
            kv_phase()
        elif stg.startswith("att"):
            att_layer(int(stg[3:]))
    norm_phase(9, final=True)
    S.emit(final_sems=["w0", "w1"])
    return nc, B.wnames


LOGG = np.log(1.0 - 2.0 ** (-5.0 - np.arange(8))).astype(np.float32).astype(np.float64)
CST_LAYOUT = {}
_o = 0
for _n, _s in (("dmaskT", (8, 128)), ("qdec", (8, 128)), ("kdec", (8,)), ("decA", (128,)), ("coefR", (64,)),
               ("maskH", (8,)), ("eps", (4,))):
    CST_LAYOUT[_n] = (_o, _s)
    _o += int(np.prod(_s))
CST_N = _o


def make_consts(core):
    c4 = core % 4
    cst = np.zeros((128, CST_N), np.float32)

    def put(name, arr):
        o, shp = CST_LAYOUT[name]
        cst[:, o:o + int(np.prod(shp))] = np.asarray(arr, np.float64).reshape(128, -1).astype(np.float32)

    idx = np.arange(128)
    diff = idx[None, :] - idx[:, None]
    dm = np.where(diff[:, None, :] >= 0, np.exp(LOGG[None, :, None] * np.maximum(diff, 0)[:, None, :]), 0.0)
    put("dmaskT", dm)
    qd = np.exp(LOGG[:, None] * (idx[None, :] + 1))
    put("qdec", np.broadcast_to(qd[None], (128, 8, 128)))
    put("kdec", np.exp(LOGG[None, :] * (127 - idx)[:, None]))
    n = np.arange(16)
    jj = n[None, :, None] * 128 + idx[:, None, None]
    put("decA", np.exp(LOGG[None, None, :] * (2047 - jj)))
    coef = np.zeros((8, 8))
    for s in range(4):
        if s < c4:
            coef[(core // 4) * 4 + s] = np.exp(LOGG * 2048.0 * (c4 - 1 - s))
    put("coefR", np.broadcast_to(coef[None], (128, 8, 8)))
    mh = np.zeros(8)
    if c4 >= 1:
        mh[core - 1] = 1.0
    put("maskH", np.broadcast_to(mh[None], (128, 8)))
    put("eps", np.full((128, 4), EPS))
    cstb = np.zeros((128, 256), np.float32)
    cstb[:, 0:128] = np.eye(128)
    cstb[:, 128:256] = 1.0
    inv = (1.0 / 10000.0 ** np.linspace(0.0, 1.0, 128)).astype(np.float32)
    pos = (c4 * TOK + np.arange(TOK)).astype(np.float32)
    ang = (pos[None, :] * inv[:, None]).astype(np.float32).astype(np.float64)
    return dict(cst=cst, cstb=cstb.astype(ml_dtypes.bfloat16), cosT=np.cos(ang).astype(np.float32),
                sinT=np.sin(ang).astype(np.float32))


def _t5_bucket(dist):
    n = np.maximum(dist, 0)
    max_exact = 16
    large = max_exact + (np.log(np.maximum(n, 1) / max_exact) / np.log(2048 / max_exact) * (32 - max_exact)).astype(np.int32)
    large = np.minimum(large, 31)
    return np.where(n < max_exact, n, large).astype(np.int32)


def make_bias(rel_bias, core):
    c4 = core % 4
    out = np.empty((16, 128, 3, 2, 256), np.float32)
    i = np.arange(128)[:, None]
    c = np.arange(256)[None, :]
    delta = 128 + i - c
    band = (delta >= 0) & (delta <= 128)
    for g, d in enumerate(DIL):
        bucket = _t5_bucket(np.maximum(delta, 0) * d)
        tab = rel_bias[:, g * 16:(g + 1) * 16][bucket]
        tab = np.moveaxis(tab, -1, 0)
        rest = np.where(band[None], tab, np.float32(-1e30))
        first = np.where((band & (c >= 128))[None], tab, np.float32(-1e30)) if c4 == 0 else rest
        out[:, :, g, 0, :] = rest
        out[:, :, g, 1, :] = first
    return out.reshape(16, 128, 6 * 256)


_CACHE = {}
SHARD_W = bool(os.environ.get('SHARD_W'))
ALL_STAGES = ('ret0', 'ffn0', 'ret1', 'ffn1', 'kv', 'att2', 'ffn2', 'att3', 'ffn3')


def kernel(x, g_mix, g_ffn, w_ret_in, w_ret_out, g_kv, w_kv, w_att_q, w_att_out, rel_bias, w_ffn_in,
           w_ffn_out, g_final, _stop_after=None, _debug_outs=()):
    f = lambda a: np.ascontiguousarray(np.asarray(a, dtype=np.float32))
    x = f(x)
    key = (tuple(_stop_after) if _stop_after else None, tuple(_debug_outs))
    if key not in _CACHE:
        _CACHE[key] = build_program(_stop_after, _debug_outs)
    nc, wnames = _CACHE[key]
    gains = np.concatenate([f(g_mix), f(g_ffn), f(g_kv)[None], f(g_final)[None]], axis=0)
    gcols = np.ascontiguousarray(gains.reshape(10, 16, 128).transpose(2, 0, 1)).reshape(128, 160)
    wts = dict(w_ret_in=f(w_ret_in), w_ret_out=f(w_ret_out), w_kv=f(w_kv)[None], w_att_q=f(w_att_q),
               w_att_out=f(w_att_out), w_ffn_in=f(w_ffn_in), w_ffn_out=f(w_ffn_out))
    rb = f(rel_bias)
    in_maps = []
    for c in range(8):
        b, c4 = c // 4, c % 4
        m = dict(gcols=gcols)
        for (name, l) in wnames:
            R = wts[name].shape[1]
            if SHARD_W:
                m[f"{name}_{l}"] = np.ascontiguousarray(wts[name][l, c * (R // 8):(c + 1) * (R // 8), :])
            else:
                m[f"{name}_{l}"] = wts[name][l]
        m["xT"] = np.ascontiguousarray(x[b, c4 * TOK:(c4 + 1) * TOK, :].T)
        m.update(make_consts(c))
        m["biasT"] = make_bias(rb, c)
        in_maps.append(m)
    res = run_bass_kernel_spmd(nc, in_maps, core_ids=list(range(8)))
    out = np.empty((2, 8192, D), np.float32)
    for c in range(8):
        b, c4 = c // 4, c % 4
        out[b, c4 * TOK:(c4 + 1) * TOK, :] = res.results[c]["outT"].T
    if _debug_outs:
        return out, res
    return out
```
